# Optimizing a Trainium2 kernel written in Bass

```python
import jax, jax.numpy as jnp
from jax import lax
import numpy as np

D_MODEL = 2048
BATCH = 8
SEQ = 2048
DEPTH = 2

N_MIXERS = 2
N_FOX = (DEPTH + 1) // 2
N_SWA = DEPTH // 2
Q_BLOCK = 128

FOX_HEADS = 16
FOX_HEAD_DIM = D_MODEL // FOX_HEADS
FOX_WIDTH = FOX_HEADS * FOX_HEAD_DIM

SWA_HEAD_DIM = 64
SWA_Q_HEADS = D_MODEL // SWA_HEAD_DIM
SWA_KV_HEADS = SWA_Q_HEADS // 8
SWA_GROUP = SWA_Q_HEADS // SWA_KV_HEADS
SWA_WINDOW = 128
ROPE_THETA = 500000.0
ROPE_DIM = SWA_HEAD_DIM // 4

D_FF = 5632
CONV_WIDTH = 3

DEEPNORM_ALPHA = (2.0 * DEPTH) ** 0.25
DEEPNORM_BETA = (8.0 * DEPTH) ** -0.25
LN_EPS = 1e-5
ADA_SCALE = 0.2
MAX_POS_OFFSET = 4096

kernel_name = "hybrid_fox_swa_sink_convffn_deepnorm_adaln"


def layer_norm(x, g, b):
    xf = x.astype(jnp.float32)
    mu = jnp.mean(xf, axis=-1, keepdims=True)
    var = jnp.mean(jnp.square(xf - mu), axis=-1, keepdims=True)
    y = (xf - mu) * lax.rsqrt(var + LN_EPS)
    return (y * g.astype(jnp.float32) + b.astype(jnp.float32)).astype(x.dtype)


def rope_partial(t, pos):
    inv_freq = ROPE_THETA ** (-jnp.arange(0, ROPE_DIM, 2, dtype=jnp.float32) / ROPE_DIM)
    ang = pos.astype(jnp.float32)[..., None] * inv_freq
    cos = jnp.cos(ang)[:, :, None, :]
    sin = jnp.sin(ang)[:, :, None, :]
    tr = t[..., :ROPE_DIM].astype(jnp.float32)
    t1, t2 = tr[..., :ROPE_DIM // 2], tr[..., ROPE_DIM // 2:]
    rot = jnp.concatenate([t1 * cos - t2 * sin, t2 * cos + t1 * sin], axis=-1)
    return jnp.concatenate([rot.astype(t.dtype), t[..., ROPE_DIM:]], axis=-1)


def fox_attention(h, w_in, b_f, w_o):
    B, S, _ = h.shape
    H, dh = FOX_HEADS, FOX_HEAD_DIM
    proj = h @ w_in
    q = proj[..., :FOX_WIDTH].reshape(B, S, H, dh)
    k = proj[..., FOX_WIDTH:2 * FOX_WIDTH].reshape(B, S, H, dh)
    v = proj[..., 2 * FOX_WIDTH:3 * FOX_WIDTH].reshape(B, S, H, dh)
    f_logit = proj[..., 3 * FOX_WIDTH:] + b_f
    log_f = jax.nn.log_sigmoid(f_logit.astype(jnp.float32))
    cum = jnp.cumsum(log_f, axis=1).transpose(0, 2, 1)
    nb = S // Q_BLOCK
    q_blocks = q.reshape(B, nb, Q_BLOCK, H, dh).transpose(1, 0, 2, 3, 4)
    cq_blocks = cum.reshape(B, H, nb, Q_BLOCK).transpose(2, 0, 1, 3)
    key_pos = jnp.arange(S)
    scale = FOX_HEAD_DIM ** -0.5

    def one_block(args):
        qb, cqb, bi = args
        s = jnp.einsum('bqhd,bkhd->bhqk', qb, k).astype(jnp.float32) * scale
        s = s + cqb[..., None] - cum[:, :, None, :]
        q_pos = bi * Q_BLOCK + jnp.arange(Q_BLOCK)
        causal = key_pos[None, :] <= q_pos[:, None]
        s = jnp.where(causal, s, -jnp.inf)
        p = jax.nn.softmax(s, axis=-1).astype(v.dtype)
        return jnp.einsum('bhqk,bkhd->bqhd', p, v)

    o = lax.map(one_block, (q_blocks, cq_blocks, jnp.arange(nb)))
    o = o.transpose(1, 0, 2, 3, 4).reshape(B, S, FOX_WIDTH)
    return o @ w_o


def swa_attention(h, pos, w_in, sinks, w_o):
    B, S, _ = h.shape
    Hq, Hk, G, dh = SWA_Q_HEADS, SWA_KV_HEADS, SWA_GROUP, SWA_HEAD_DIM
    proj = h @ w_in
    q = proj[..., :Hq * dh].reshape(B, S, Hq, dh)
    k = proj[..., Hq * dh:(Hq + Hk) * dh].reshape(B, S, Hk, dh)
    v = proj[..., (Hq + Hk) * dh:].reshape(B, S, Hk, dh)
    q = rope_partial(q, pos)
    k = rope_partial(k, pos)
    nb = S // Q_BLOCK
    qb = q.reshape(B, nb, Q_BLOCK, Hk, G, dh)

    def band(t):
        tb = t.reshape(B, nb, Q_BLOCK, Hk, dh)
        prev = jnp.concatenate([jnp.zeros_like(tb[:, :1]), tb[:, :-1]], axis=1)
        return jnp.concatenate([prev, tb], axis=2)

    kb, vb = band(k), band(v)
    s = jnp.einsum('bnqhgd,bnkhd->bnhgqk', qb, kb).astype(jnp.float32) * (dh ** -0.5)
    qi = jnp.arange(Q_BLOCK)[:, None]
    kj = jnp.arange(2 * Q_BLOCK)[None, :]
    rel = qi + Q_BLOCK - kj
    key_abs = (jnp.arange(nb) * Q_BLOCK)[:, None] - Q_BLOCK + kj
    mask = (rel >= 0)[None] & (rel < SWA_WINDOW)[None] & (key_abs[:, None, :] >= 0)
    s = jnp.where(mask[None, :, None, None], s, -jnp.inf)
    sink = jnp.broadcast_to(sinks.astype(jnp.float32).reshape(1, 1, Hk, G, 1, 1), s.shape[:-1] + (1,))
    p = jax.nn.softmax(jnp.concatenate([s, sink], axis=-1), axis=-1)[..., :-1].astype(v.dtype)
    o = jnp.einsum('bnhgqk,bnkhd->bnqhgd', p, vb).reshape(B, S, Hq * dh)
    return o @ w_o


def conv_ffn(h, w_up, conv_w, conv_b, w_down):
    S = h.shape[1]
    u = h @ w_up
    up = jnp.pad(u, ((0, 0), (CONV_WIDTH - 1, 0), (0, 0)))
    u = sum(up[:, j:j + S] * conv_w[j] for j in range(CONV_WIDTH)) + conv_b
    g, val = u[..., :D_FF], u[..., D_FF:]
    return (jax.nn.silu(g) * val) @ w_down


def setup_inputs(seed: int = 0) -> dict:
    key = jax.random.key(seed)
    ks = jax.random.split(key, 20)
    f32 = jnp.float32
    n = lambda k, shape, s: (jax.random.normal(k, shape, f32) * s)
    D = D_MODEL
    x = n(ks[0], (BATCH, SEQ, D), 1.0)
    c = n(ks[1], (BATCH, D), 1.0)
    offset = jax.random.randint(ks[2], (BATCH, 1), 0, MAX_POS_OFFSET, dtype=jnp.int32)
    positions = (offset + jnp.arange(SEQ, dtype=jnp.int32)[None, :]).astype(jnp.int32)
    fox_w_in = n(ks[3], (N_FOX, D, 3 * FOX_WIDTH + FOX_HEADS), D ** -0.5)
    fox_b_f = n(ks[4], (N_FOX, FOX_HEADS), 0.1)
    fox_w_o = n(ks[5], (N_FOX, FOX_WIDTH, D), FOX_WIDTH ** -0.5 * DEEPNORM_BETA)
    swa_w_in = n(ks[6], (N_SWA, D, (SWA_Q_HEADS + 2 * SWA_KV_HEADS) * SWA_HEAD_DIM), D ** -0.5)
    swa_sinks = n(ks[7], (N_SWA, SWA_Q_HEADS), 0.5)
    swa_w_o = n(ks[8], (N_SWA, SWA_Q_HEADS * SWA_HEAD_DIM, D), (SWA_Q_HEADS * SWA_HEAD_DIM) ** -0.5 * DEEPNORM_BETA)
    ada_w = n(ks[9], (DEPTH, D, 6 * D), ADA_SCALE * D ** -0.5)
    ada_b = n(ks[10], (DEPTH, 6 * D), 0.02)
    ffn_w_up = n(ks[11], (DEPTH, D, 2 * D_FF), D ** -0.5)
    ffn_conv_w = n(ks[12], (DEPTH, CONV_WIDTH, 2 * D_FF), CONV_WIDTH ** -0.5)
    ffn_conv_b = n(ks[13], (DEPTH, 2 * D_FF), 0.02)
    ffn_w_down = n(ks[14], (DEPTH, D_FF, D), D_FF ** -0.5 * DEEPNORM_BETA)
    ln_mix_g = 1.0 + n(ks[15], (DEPTH, D), 0.02)
    ln_mix_b = n(ks[16], (DEPTH, D), 0.02)
    ln_ffn_g = 1.0 + n(ks[17], (DEPTH, D), 0.02)
    ln_ffn_b = n(ks[18], (DEPTH, D), 0.02)
    return {"x": x, "c": c, "positions": positions,
            "fox_w_in": fox_w_in, "fox_b_f": fox_b_f, "fox_w_o": fox_w_o,
            "swa_w_in": swa_w_in, "swa_sinks": swa_sinks, "swa_w_o": swa_w_o,
            "ada_w": ada_w, "ada_b": ada_b,
            "ffn_w_up": ffn_w_up, "ffn_conv_w": ffn_conv_w, "ffn_conv_b": ffn_conv_b, "ffn_w_down": ffn_w_down,
            "ln_mix_g": ln_mix_g, "ln_mix_b": ln_mix_b, "ln_ffn_g": ln_ffn_g, "ln_ffn_b": ln_ffn_b}


def reference(x, c, positions, fox_w_in, fox_b_f, fox_w_o, swa_w_in, swa_sinks, swa_w_o,
              ada_w, ada_b, ffn_w_up, ffn_conv_w, ffn_conv_b, ffn_w_down,
              ln_mix_g, ln_mix_b, ln_ffn_g, ln_ffn_b):
    c_act = jax.nn.silu(c)
    for i in range(DEPTH):
        mod = c_act @ ada_w[i] + ada_b[i]
        sh1, sc1, g1, sh2, sc2, g2 = jnp.split(mod[:, None, :], 6, axis=-1)
        h = x * (1.0 + sc1) + sh1
        j = i // N_MIXERS
        if i % N_MIXERS == 0:
            y = fox_attention(h, fox_w_in[j], fox_b_f[j], fox_w_o[j])
        else:
            y = swa_attention(h, positions, swa_w_in[j], swa_sinks[j], swa_w_o[j])
        x = layer_norm(DEEPNORM_ALPHA * x + (1.0 + g1) * y, ln_mix_g[i], ln_mix_b[i])
        h = x * (1.0 + sc2) + sh2
        y = conv_ffn(h, ffn_w_up[i], ffn_conv_w[i], ffn_conv_b[i], ffn_w_down[i])
        x = layer_norm(DEEPNORM_ALPHA * x + (1.0 + g2) * y, ln_ffn_g[i], ln_ffn_b[i])
    return x
```

```python
import numpy as np
from contextlib import ExitStack
import concourse.bass as bass
import concourse.mybir as mybir
from concourse.bass_utils import run_bass_kernel_spmd

F32 = mybir.dt.float32
BF16 = mybir.dt.bfloat16
I32 = mybir.dt.int32
AF = mybir.ActivationFunctionType
ALU = mybir.AluOpType

S = 2048
D = 2048
DC = 16
NT = 4
TT = 512
DFF = 5632
FC = 44
ALPHA = 4.0 ** 0.25
LN_EPS = 1e-5
NEG = -30000.0


class Buf:
    __slots__ = ("name", "w", "r", "dsem", "dcnt", "excl")

    def __init__(self, name):
        self.name = name
        self.excl = False
        self.w = None
        self.r = {}
        self.dsem = None
        self.dcnt = 0


class Ker:
    def __init__(self, nc):
        self.nc = nc
        self.E = {"pe": nc.tensor, "act": nc.scalar, "dve": nc.vector, "pool": nc.gpsimd, "sp": nc.sync}
        self.semh = {}
        self.cnt = {}
        for e in ("pe", "act", "dve", "pool"):
            self.semh[e] = nc.alloc_semaphore("c_" + e)
            self.cnt[e] = 0
        self.seen = {e: {} for e in self.E}
        self.nbuf = 0

    def buf(self, name=None):
        self.nbuf += 1
        return Buf(name or ("b%d" % self.nbuf))

    def bufs(self, n, name="b"):
        return [self.buf("%s%d_%d" % (name, self.nbuf, i)) for i in range(n)]

    def _wait(self, eng, tok, raw):
        if tok is None:
            return
        key, val = tok
        if key == eng and eng == "pe":
            return
        if self.seen[eng].get(key, 0) >= val:
            return
        self.E[eng].wait_ge(self.semh[key], val)
        self.seen[eng][key] = val

    def _deps(self, eng, reads, writes):
        for b in reads:
            self._wait(eng, b.w, True)
            if b.excl:
                for key, val in b.r.items():
                    if key != eng:
                        self._wait(eng, (key, val), True)
        for b in writes:
            self._wait(eng, b.w, False)
            for key, val in b.r.items():
                self._wait(eng, (key, val), False)

    def _mark(self, tok, reads, writes):
        key, val = tok
        for b in reads:
            if b.r.get(key, 0) < val:
                b.r[key] = val
        for b in writes:
            b.w = tok
            b.r = {}

    def op(self, eng, fn, reads=(), writes=()):
        self._deps(eng, reads, writes)
        ins = fn(self.E[eng])
        self.cnt[eng] += 1
        ins.then_inc(self.semh[eng], 1)
        tok = (eng, self.cnt[eng])
        self._mark(tok, reads, writes)
        return tok

    def dma(self, q, out, in_, sb, reads=(), writes=()):
        kind = "sw" if q == "pool" else "hw"
        if sb.dsem is None:
            sb.dsem = {}
            sb.dcnt = {}
        for kd, cnt in sb.dcnt.items():
            if cnt:
                self._wait(q, ("d:%s:%s" % (sb.name, kd), 16 * cnt), True)
        key = "d:%s:%s" % (sb.name, kind)
        if kind not in sb.dsem:
            sb.dsem[kind] = self.nc.alloc_semaphore("d_%s_%s" % (sb.name, kind))
            sb.dcnt[kind] = 0
            self.semh[key] = sb.dsem[kind]
        self._deps(q, reads, writes)
        ins = self.E[q].dma_start(out=out, in_=in_)
        sb.dcnt[kind] += 1
        ins.then_inc(sb.dsem[kind], 16)
        tok = (key, 16 * sb.dcnt[kind])
        self._mark(tok, reads, writes)
        return tok

    def dma_done(self, eng, sb):
        if sb.dcnt:
            for kd, cnt in sb.dcnt.items():
                if cnt:
                    self._wait(eng, ("d:%s:%s" % (sb.name, kd), 16 * cnt), True)

    def wait_all(self, eng, bufs):
        for b in bufs:
            self._wait(eng, b.w, True)
            for key, val in b.r.items():
                self._wait(eng, (key, val), True)


def kmajor(w):
    K, C = w.shape
    return np.ascontiguousarray(w.reshape(K // 128, 128, C).transpose(1, 0, 2))


def units(w, ucols):
    K, C = w.shape
    nu = C // ucols
    return np.ascontiguousarray(w.reshape(K // 128, 128, nu, ucols).transpose(2, 1, 0, 3))


def pvec(v):
    n = v.shape[0] // 128
    return np.ascontiguousarray(v.reshape(n, 128).T)


def host_consts():
    c = {}
    c["ident"] = np.eye(128, dtype=np.float32)
    p = np.arange(128)[:, None]
    i = np.arange(128)[None, :]
    c["maskc"] = np.where(p <= i, 0.0, NEG).astype(np.float32)
    c["masks"] = np.concatenate([np.where(p <= i, 0.0, NEG), np.where(p > i, 0.0, NEG)], axis=1).astype(np.float32)
    pm = np.zeros((128, 128), np.float32)
    invf = np.zeros((128, 1), np.float32)
    sgn = np.zeros((128, 1), np.float32)
    rot = np.zeros((128, 1), np.float32)
    inv_freq = (500000.0 ** (-np.arange(0, 16, 2, dtype=np.float32) / 16)).astype(np.float32)
    for m in range(128):
        d = m % 64
        if d < 8:
            pm[m + 8, m] = 1.0
            invf[m] = inv_freq[d]
            sgn[m] = -1.0
            rot[m] = 1.0
        elif d < 16:
            pm[m - 8, m] = 1.0
            invf[m] = inv_freq[d - 8]
            sgn[m] = 1.0
            rot[m] = 1.0
    c["pmat"] = pm
    c["ropec"] = np.concatenate([invf, sgn, rot, 1.0 - rot], axis=1).astype(np.float32)
    return c


def prep_inputs(inp):
    sh = {}
    sh.update(host_consts())
    for l in range(2):
        au = units(np.asarray(inp["ada_w"][l]), 512)
        for q in range(4):
            sh["ada_r%d%d" % (l, q)] = np.ascontiguousarray(au[q * 6:(q + 1) * 6])
    sh["ada_bT"] = np.concatenate([pvec(np.asarray(inp["ada_b"][l])) for l in range(2)], axis=1)
    fw = np.asarray(inp["fox_w_in"][0])
    fu = units(fw[:, :6144], 512)
    sh["fox_in_r0"] = np.ascontiguousarray(fu[:6])
    sh["fox_in_r1"] = np.ascontiguousarray(fu[6:])
    sh["fox_f_r"] = kmajor(fw[:, 6144:6160])
    sh["fox_bf"] = np.asarray(inp["fox_b_f"][0]).reshape(16, 1).astype(np.float32)
    sh["fox_o_r"] = units(np.asarray(inp["fox_w_o"][0]), 512)
    sw = np.asarray(inp["swa_w_in"][0])
    kd = np.concatenate([np.concatenate([sw[:, 2048 + 64 * g:2048 + 64 * g + 64]] * 2, axis=1) for g in range(4)], axis=1)
    vd = np.concatenate([np.concatenate([sw[:, 2304 + 64 * g:2304 + 64 * g + 64]] * 2, axis=1) for g in range(4)], axis=1)
    sh["swa_in_r"] = units(np.concatenate([sw[:, :2048], kd, vd], axis=1), 512)
    sh["swa_sk"] = np.ascontiguousarray(np.broadcast_to(np.asarray(inp["swa_sinks"][0]).reshape(1, 32), (128, 32))).astype(np.float32)
    sh["swa_o_r"] = units(np.asarray(inp["swa_w_o"][0]), 512)
    ups, dns, ctab = [], [], []
    for l in range(2):
        wu = np.asarray(inp["ffn_w_up"][l])
        g = wu[:, :DFF].reshape(D, 44, 128)
        v = wu[:, DFF:].reshape(D, 44, 128)
        uu = np.stack([g, v], axis=2).reshape(D, 44 * 256)
        ups.append(units(uu, 256))
        wd = np.asarray(inp["ffn_w_down"][l])
        dns.append(np.ascontiguousarray(wd.reshape(FC, 128, 16, 128).transpose(2, 1, 0, 3)))
        cw = np.asarray(inp["ffn_conv_w"][l])
        cb = np.asarray(inp["ffn_conv_b"][l])
        t4 = np.stack([cw[0], cw[1], cw[2], cb], axis=1)
        ctab.append(np.ascontiguousarray(t4.reshape(88, 128, 4).transpose(1, 0, 2)))
    for l in range(2):
        for q in range(4):
            sh["up_r%d%d" % (l, q)] = np.ascontiguousarray(ups[l][q * 11:(q + 1) * 11])
        for q in range(2):
            sh["dn_r%d%d" % (l, q)] = np.ascontiguousarray(dns[l][q * 8:(q + 1) * 8])
    sh["ctab"] = np.stack(ctab)
    sh["lnp"] = np.concatenate(
        [pvec(np.asarray(inp[nm][l])) for l in range(2) for nm in ("ln_mix_g", "ln_mix_b", "ln_ffn_g", "ln_ffn_b")], axis=1
    ).astype(np.float32)
    per = []
    x = np.asarray(inp["x"])
    c = np.asarray(inp["c"])
    pos = np.asarray(inp["positions"])
    for b in range(8):
        per.append({
            "x": np.ascontiguousarray(x[b]),
            "cT": pvec(c[b]),
            "posb": np.ascontiguousarray(np.broadcast_to(pos[b].reshape(1, S), (128, S))).astype(np.int32),
        })
    return sh, per


SHAPES = {
    "ident": ([128, 128], F32), "maskc": ([128, 128], F32), "masks": ([128, 256], F32),
"pmat": ([128, 128], F32), "ropec": ([128, 4], F32),
"ada_bT": ([128, 192], F32),
    "fox_in_r0": ([6, 128, 16, 512], F32), "fox_in_r1": ([6, 128, 16, 512], F32), "fox_f_r": ([128, 16, 16], F32), "fox_bf": ([16, 1], F32),
    "fox_o_r": ([4, 128, 16, 512], F32),
    "swa_in_r": ([6, 128, 16, 512], F32), "swa_sk": ([128, 32], F32),
    "swa_o_r": ([4, 128, 16, 512], F32),
"ctab": ([2, 128, 88, 4], F32),
    "lnp": ([128, 128], F32),
    "x": ([S, D], F32), "cT": ([128, 16], F32), "posb": ([128, S], I32),
}
for _l in range(2):
    for _q in range(4):
        SHAPES["ada_r%d%d" % (_l, _q)] = ([6, 128, 16, 512], F32)
        SHAPES["up_r%d%d" % (_l, _q)] = ([11, 128, 16, 256], F32)
    for _q in range(2):
        SHAPES["dn_r%d%d" % (_l, _q)] = ([8, 128, 44, 128], F32)


class Prog:
    def __init__(self, stop=99, dbg=()):
        self.stop = stop
        self.dbg = dbg
        nc = self.nc = bass.Bass("TRN2", target_bir_lowering=False)
        self.T = {}
        for name, (shape, dt) in SHAPES.items():
            self.T[name] = nc.dram_tensor(name, shape, dt, kind="ExternalInput").ap()
        self.y = nc.dram_tensor("y", [S, D], F32, kind="ExternalOutput").ap()
        self.HD = nc.dram_tensor("HD", [16, 128, S], BF16, kind="Internal").ap()
        self.XA = nc.dram_tensor("XA", [16, 128, S], F32, kind="Internal").ap()
        self.OD = nc.dram_tensor("OD", [16, 128, S], BF16, kind="Internal").ap()
        self.UPB = nc.dram_tensor("UPB", [2, 44, 128, 16 * 256], BF16, kind="Internal").ap()
        self.DNB = nc.dram_tensor("DNB", [2, 16, 128, 44 * 128], BF16, kind="Internal").ap()
        self.k = Ker(nc)
        k = self.k
        A = lambda n, sh, dt: nc.alloc_sbuf_tensor('s_' + n, sh, dt)
        self.ident = A("ident", [128, 128], F32)
        self.maskc = A("maskc", [128, 128], F32)
        self.masks = A("masks", [128, 256], F32)
        self.pmat = A("pmat", [128, 128], BF16)
        self.pmat32 = A("pmat32", [128, 128], F32)
        self.ones_bf = A("ones_bf", [128, 128], BF16)
        self.ropec = A("ropec", [128, 4], F32)
        self.lnp = A("lnp", [128, 128], F32)
        self.adab = A("adab", [128, 192], F32)
        self.mod = A("mod", [128, 192], F32)
        self.vec = A("vec", [128, 2 * 16 * 16], F32)
        self.ctab = A("ctab", [128, 2 * 88 * 4], F32)
        self.c32 = A("c32", [128, 16], F32)
        self.cact = A("cact", [128, 16], BF16)
        self.cstB = k.buf("cst")
        self.vecB = k.buf("vec")
        self.RING = A("ring", [128, 12288], F32)
        self.BIGA = A("biga", [128, 16384], F32)
        self.BIGB = A("bigb", [128, 12288], F32)
        self.WORK = A("work", [128, 9216], F32)
        self.PS = nc.alloc_psum_tensor("ps", [128, 8, 512], F32)
        self.psB = k.bufs(8, "ps")
        for b in self.psB:
            b.excl = True
        self.rsB = k.bufs(3, "rs")
        self.rs_i = 0
        self.rot_i = 0
        self.XAb = [[k.buf("xa%d_%d" % (c, t)) for t in range(NT)] for c in range(DC)]
        self.HDb = [[k.buf("hd%d_%d" % (c, t)) for t in range(NT)] for c in range(DC)]
        self.ODb = [[k.buf("od%d_%d" % (c, t)) for t in range(NT)] for c in range(DC)]
        self.UPb = [[k.buf("up%d_%d" % (l, u)) for u in range(44)] for l in range(2)]
        self.DNb = [[k.buf("dn%d_%d" % (l, u)) for u in range(16)] for l in range(2)]
        self.precast = []
        for l in range(2):
            for q in range(4):
                self.precast.append(("up", l, q))
            for q in range(2):
                self.precast.append(("dn", l, q))
        self.pc_sems = k.bufs(4, "pcast")
        self.pc_i = 0
        self.allbufs = []
        self.bigaB = []
        self.bigbB = []
        self.workB = []

    def bfview(self, reg, off_f32, shape):
        n = shape[0] * shape[1]
        ap = reg[:, off_f32:off_f32 + n // 2].bitcast(BF16)
        return ap.rearrange("p (a b) -> p a b", b=shape[1])

    def f32view(self, reg, off, shape):
        n = shape[0] * shape[1]
        return reg[:, off:off + n].rearrange("p (a b) -> p a b", b=shape[1])

    def inherit(self, new, old):
        for nb in new:
            for ob in old:
                toks = dict(ob.r)
                if ob.w is not None:
                    kk, vv = ob.w
                    if toks.get(kk, 0) < vv:
                        toks[kk] = vv
                for kk, vv in toks.items():
                    if nb.r.get(kk, 0) < vv:
                        nb.r[kk] = vv

    def vslice(self, l, name):
        names = ["sh1", "sc1p", "g1p", "sh2", "sc2p", "g2p", "A1", "B1", "G2", "H2", "A2", "B2", "G1n", "H1n", "tmp", "tmp2"]
        i = names.index(name)
        o = (l * 16 + i) * 16
        return self.vec[:, o:o + 16]

    def lnslice(self, l, i):
        o = (l * 4 + i) * 16
        return self.lnp[:, o:o + 16]

    def do_precast(self, n=1):
        k = self.k
        for _ in range(n):
            if not self.precast:
                return
            kind, l, u = self.precast.pop(0)
            self.pc_sem = self.pc_sems[self.pc_i % 4]
            self.pc_i += 1
            if kind == "up":
                k.dma("pool", self.UPB[l, u * 11:(u + 1) * 11], self.T["up_r%d%d" % (l, u)].rearrange("u p a b -> u p (a b)"),
                      self.pc_sem, writes=self.UPb[l][u * 11:(u + 1) * 11])
            else:
                k.dma("pool", self.DNB[l, u * 8:(u + 1) * 8], self.T["dn_r%d%d" % (l, u)].rearrange("u p a b -> u p (a b)"),
                      self.pc_sem, writes=self.DNb[l][u * 8:(u + 1) * 8])

    def ring_slot(self):
        i = self.rs_i % 3
        self.rs_i += 1
        return i, self.rsB[i]

    def load_w(self, src, shape, q="pool"):
        i, b = self.ring_slot()
        view = self.bfview(self.RING, i * 4096, shape)
        if q == "pool":
            self.do_precast(1)
        self.k.dma(q, view, src, b, writes=[b])
        return view, b

    def rot(self):
        i = self.rot_i % 3
        self.rot_i += 1
        return self.PS[:, i, :], self.psB[i]

    def stage_consts(self):
        k, T = self.k, self.T
        cb = self.cstB
        for dst, nm in ((self.ident, "ident"), (self.maskc, "maskc"), (self.masks, "masks"), (self.pmat32, "pmat"),
                        (self.ropec, "ropec"), (self.lnp, "lnp"), (self.adab, "ada_bT"), (self.c32, "cT")):
            k.dma("sp", dst[:], T[nm], cb, writes=[cb])
        k.dma("sp", self.ctab[:].rearrange("p (l a) -> p l a", l=2), T["ctab"].rearrange("l p a b -> p l (a b)"), cb, writes=[cb])
        k.op("dve", lambda e: e.memset(self.ones_bf[:], 1.0), writes=[cb])
        k.op("dve", lambda e: e.tensor_copy(out=self.pmat[:], in_=self.pmat32[:]), reads=[cb], writes=[cb])
        k.op("act", lambda e: e.activation(out=self.cact[:], in_=self.c32[:], func=AF.Silu), reads=[cb], writes=[cb])

    def stage_mod(self, l):
        k, T = self.k, self.T
        mps = self.PS[:, 7, :]
        mpb = self.psB[7]
        for u in range(24):
            w, wb = self.load_w(T["ada_r%d%d" % (l, u // 6)][u % 6], (16, 512))

            def grp(e, w=w, u=u):
                ins = None
                for j in range(4):
                    col = l * 96 + u * 4 + j
                    for kc in range(16):
                        ins = e.matmul(mps[:, col:col + 1], lhsT=w[:, kc, j * 128:(j + 1) * 128], rhs=self.cact[:, kc:kc + 1],
                                       start=(kc == 0), stop=(kc == 15))
                return ins
            k.op("pe", grp, reads=[wb, self.cstB], writes=[mpb])
        sl = slice(l * 96, (l + 1) * 96)
        vb = self.vecB
        k.op("dve", lambda e: e.tensor_tensor(out=self.mod[:, sl], in0=mps[:, sl], in1=self.adab[:, sl], op=ALU.add),
             reads=[mpb, self.cstB], writes=[vb])

        def m(i):
            o = l * 96 + i * 16
            return self.mod[:, o:o + 16]
        V = lambda n: self.vslice(l, n)
        dv = lambda fn: k.op("dve", fn, reads=[vb, self.cstB], writes=[vb])
        dv(lambda e: e.tensor_copy(out=V("sh1"), in_=m(0)))
        dv(lambda e: e.tensor_scalar(out=V("sc1p"), in0=m(1), scalar1=1.0, scalar2=None, op0=ALU.add))
        dv(lambda e: e.tensor_scalar(out=V("g1p"), in0=m(2), scalar1=1.0, scalar2=None, op0=ALU.add))
        dv(lambda e: e.tensor_copy(out=V("sh2"), in_=m(3)))
        dv(lambda e: e.tensor_scalar(out=V("sc2p"), in0=m(4), scalar1=1.0, scalar2=None, op0=ALU.add))
        dv(lambda e: e.tensor_scalar(out=V("g2p"), in0=m(5), scalar1=1.0, scalar2=None, op0=ALU.add))
        mg, mb, fg, fb = (self.lnslice(l, i) for i in range(4))
        dv(lambda e: e.tensor_scalar(out=V("A1"), in0=mg, scalar1=ALPHA, scalar2=None, op0=ALU.mult))
        dv(lambda e: e.tensor_scalar(out=V("B1"), in0=mb, scalar1=ALPHA, scalar2=None, op0=ALU.mult))
        dv(lambda e: e.tensor_tensor(out=V("G2"), in0=mg, in1=V("sc2p"), op=ALU.mult))
        dv(lambda e: e.tensor_tensor(out=V("tmp"), in0=mb, in1=V("sc2p"), op=ALU.mult))
        dv(lambda e: e.tensor_tensor(out=V("H2"), in0=V("tmp"), in1=V("sh2"), op=ALU.add))
        dv(lambda e: e.tensor_scalar(out=V("A2"), in0=fg, scalar1=ALPHA, scalar2=None, op0=ALU.mult))
        dv(lambda e: e.tensor_scalar(out=V("B2"), in0=fb, scalar1=ALPHA, scalar2=None, op0=ALU.mult))
        if l == 1:
            fg0, fb0 = self.lnslice(0, 2), self.lnslice(0, 3)
            dv(lambda e: e.tensor_tensor(out=self.vslice(0, "G1n"), in0=fg0, in1=V("sc1p"), op=ALU.mult))
            dv(lambda e: e.tensor_tensor(out=self.vslice(0, "tmp2"), in0=fb0, in1=V("sc1p"), op=ALU.mult))
            dv(lambda e: e.tensor_tensor(out=self.vslice(0, "H1n"), in0=self.vslice(0, "tmp2"), in1=V("sh1"), op=ALU.add))
            dv(lambda e: e.tensor_copy(out=V("G1n"), in_=fg))
            dv(lambda e: e.tensor_copy(out=V("H1n"), in_=fb))

    def dump(self, name, ap, shape, dt, reads):
        d = self.nc.dram_tensor("dbg_" + name, shape, dt, kind="ExternalOutput").ap()
        b = self.k.buf("dbg_" + name)
        self.k.dma("sp", d, ap, b, reads=reads)
        self.allbufs.append(b)

    def dump3(self, name, ap, shape, dt, reads):
        d = self.nc.dram_tensor("dbg_" + name, shape, dt, kind="ExternalOutput").ap()
        b = self.k.buf("dbg_" + name)
        for c in range(shape[0]):
            self.k.dma("sp", d[c], ap[c], b, reads=reads)
        self.allbufs.append(b)

    def finish(self):
        k = self.k
        for b in self.allbufs:
            k.wait_all("sp", [b])
            k.dma_done("sp", b)

    def stage_pro(self, do_act=True, do_dve=True, ntt=NT):
        k, T = self.k, self.T
        XS = self.f32view(self.BIGB, 0, (4, 2048))
        xsB = k.bufs(4, "xs")
        self.inherit(xsB, self.bigbB)
        self.bigbB = xsB
        H = self.bfview(self.BIGA, 0, (16, 2048))
        self.Hb = [[k.buf() for t in range(NT)] for c in range(DC)]
        self.bigaB = [b for row in self.Hb for b in row]
        stg = [self.WORK[:, i * 512:(i + 1) * 512] for i in range(3)]
        stgB = k.bufs(3, "stg")
        self.workB = stgB
        for tt in range(ntt):
            tsl = slice(tt * 512, (tt + 1) * 512)
            for s in range(4):
                r0 = (tt * 4 + s) * 128
                k.dma("sp", XS[:, s, :], T["x"][r0:r0 + 128, :], xsB[s], writes=[xsB[s]])
            for c in range(DC):
                ps, pb = self.rot()

                def grp(e):
                    ins = None
                    for s in range(4):
                        ins = e.transpose(out=ps[:, s * 128:(s + 1) * 128], in_=XS[:, s, c * 128:(c + 1) * 128], identity=self.ident[:])
                    return ins
                k.op("pe", grp, reads=xsB + [self.cstB], writes=[pb])
                sc = self.vslice(0, "sc1p")[:, c:c + 1]
                shv = self.vslice(0, "sh1")[:, c:c + 1]
                if do_act:
                    k.op("act", lambda e: e.activation(out=H[:, c, tsl], in_=ps, func=AF.Identity, bias=shv, scale=sc),
                         reads=[pb, self.vecB], writes=[self.Hb[c][tt]])
                i = (tt * 16 + c) % 3
                if do_dve:
                    k.op("dve", lambda e: e.tensor_scalar(out=stg[i], in0=ps, scalar1=ALPHA, scalar2=None, op0=ALU.mult),
                         reads=[pb], writes=[stgB[i]])
                    k.dma("sp", self.XA[c, :, tsl], stg[i], stgB[i], reads=[stgB[i]], writes=[self.XAb[c][tt]])

    def stage_loadH(self):
        k = self.k
        H = self.bfview(self.BIGA, 0, (16, 2048))
        self.Hb = [[k.buf() for t in range(NT)] for c in range(DC)]
        newb = [b for row in self.Hb for b in row]
        self.inherit(newb, self.bigaB)
        self.bigaB = newb
        for c in range(DC):
            k.dma("sp", H[:, c, :], self.HD[c], self.Hb[c][0], reads=self.HDb[c], writes=self.Hb[c])

    def stage_fox(self):
        k, T = self.k, self.T
        H = self.bfview(self.BIGA, 0, (16, 2048))
        Hb = self.Hb
        Ht = lambda t: [Hb[c][t] for c in range(DC)]
        Q = self.bfview(self.BIGB, 0, (4, 2048))
        Kt = self.bfview(self.BIGB, 4096, (4, 2048))
        V = self.bfview(self.BIGB, 8192, (16, 512))
        Qb = [[k.buf() for t in range(NT)] for j in range(4)]
        Kb = [[k.buf() for t in range(NT)] for j in range(4)]
        Vb = [k.buf() for t in range(16)]
        newb = [b for r in Qb for b in r] + [b for r in Kb for b in r] + Vb
        self.inherit(newb, self.bigbB)
        self.bigbB = newb
        W = self.WORK
        CQ = [W[:, 0:512], W[:, 512:1024]]
        SX = [W[:, 1024 + i * 512:1536 + i * 512] for i in range(3)]
        PT = [W[:, 2560 + i * 256:2816 + i * 256].bitcast(BF16) for i in range(3)]
        R = W[:, 3328:3840]
        OST = [W[:, 3840 + i * 256:4096 + i * 256].bitcast(BF16) for i in range(2)]
        CUM = W[0:16, 4352:6400]
        Ee = W[0:16, 6400:6912]
        ONES16 = W[0:16, 6912:7424]
        OH = [W[0:16, 7424:7936], W[0:16, 7936:8448]]
        CK = W[:, 8448:8704]
        NBF = W[0:16, 8704:8705]
        ONEHOT = W[0:16, 8720:8736]
        ONES16F = W[0:16, 8736:8864]
        WF = W[:, 8864:8992].bitcast(BF16).rearrange("p (a b) -> p a b", b=16)
        cqB, sxB, ptB, ostB, ohB = k.bufs(2, "cq"), k.bufs(3, "sx"), k.bufs(3, "pt"), k.bufs(2, "ost"), k.bufs(2, "oh")
        rB, cumB, eB, gB, wfB = k.buf("r"), k.buf("cum"), k.buf("e"), k.buf("gc"), k.buf("wf")
        neww = cqB + sxB + ptB + ostB + ohB + [rB, cumB, eB, gB, wfB]
        self.inherit(neww, self.workB)
        self.workB = neww
        k.dma("sp", NBF, T["fox_bf"], gB, writes=[gB])
        k.op("dve", lambda e: e.tensor_scalar(out=NBF, in0=NBF, scalar1=-1.0, scalar2=None, op0=ALU.mult), reads=[gB], writes=[gB])
        k.op("dve", lambda e: e.memset(ONES16, 1.0), writes=[gB])
        k.op("dve", lambda e: e.memset(ONES16F, 1.0), writes=[gB])
        k.op("dve", lambda e: e.tensor_scalar(out=ONEHOT, in0=self.ident[0:16, 0:16], scalar1=-1.0, scalar2=None, op0=ALU.mult),
             reads=[self.cstB], writes=[gB])
        k.dma("pool", WF, T["fox_f_r"], wfB, writes=[wfB])
        ps7, pb7 = self.PS[:, 7, :], self.psB[7]
        for tt in range(NT):
            tsl = slice(tt * 512, (tt + 1) * 512)

            def grp(e):
                ins = None
                for kc in range(16):
                    ins = e.matmul(ps7[0:16, :], lhsT=WF[:, kc, :], rhs=H[:, kc, tsl], start=(kc == 0), stop=(kc == 15))
                return ins
            k.op("pe", grp, reads=Ht(tt) + [wfB], writes=[pb7])
            k.op("act", lambda e: e.activation(out=Ee, in_=ps7[0:16, :], func=AF.Exp, bias=NBF, scale=-1.0), reads=[pb7, gB], writes=[eB])
            k.op("act", lambda e: e.activation(out=OH[0], in_=Ee, func=AF.Ln, bias=1.0, scale=1.0), reads=[eB], writes=[ohB[0]])
            init = 0.0 if tt == 0 else CUM[:, tt * 512 - 1:tt * 512]
            k.op("dve", lambda e: e.tensor_tensor_scan(out=CUM[:, tsl], data0=ONES16, data1=OH[0], initial=init, op0=ALU.mult, op1=ALU.add),
                 reads=[ohB[0], gB, cumB], writes=[cumB])

        def grp(e):
            ins = None
            for kc in range(16):
                ins = e.transpose(out=ps7[:, kc * 16:(kc + 1) * 16], in_=CUM[:, kc * 128:(kc + 1) * 128], identity=self.ident[0:16, 0:16])
            return ins
        k.op("pe", grp, reads=[cumB, self.cstB], writes=[pb7])
        ckB = k.buf("ck")
        self.workB.append(ckB)
        k.op("dve", lambda e: e.tensor_copy(out=CK, in_=ps7[:, 0:256]), reads=[pb7], writes=[ckB])

        pend = []
        state = {"ti": 0, "tile": 0}

        def emit_pv(t):
            i, h, j, qt, kc, j0, last, oi = t
            cols = slice(j0 * 128, 512)
            O, oB = self.PS[:, 3 + oi, :], self.psB[3 + oi]
            DEN, dB = self.PS[:, 5 + oi, :], self.psB[5 + oi]

            def g2(e):
                e.matmul(O[:, cols], lhsT=V[:, kc, j * 128:(j + 1) * 128], rhs=PT[i][:, cols], start=(kc == 0), stop=last)
                return e.matmul(DEN[:, cols], lhsT=self.ones_bf[:], rhs=PT[i][:, cols], start=(kc == 0), stop=last)
            k.op("pe", g2, reads=[ptB[i], Vb[kc], self.cstB], writes=[oB, dB])
            if last:
                k.op("dve", lambda e: e.reciprocal(out=R, in_=DEN), reads=[dB], writes=[rB])
                oo = state["tile"] % 2
                state["tile"] += 1
                k.op("dve", lambda e: e.tensor_tensor(out=OST[oo], in0=O, in1=R, op=ALU.mult), reads=[oB, rB], writes=[ostB[oo]])
                k.dma("sp", self.OD[h, :, qt * 512:(qt + 1) * 512], OST[oo], ostB[oo], reads=[ostB[oo]], writes=[self.ODb[h][qt]])

        for grpi in range(4):
            for kind in range(2):
                w, wb = self.load_w(T["fox_in_r%d" % ((kind * 4 + grpi) // 6)][(kind * 4 + grpi) % 6], (16, 512))
                for j in range(4):
                    for tt in range(NT):
                        tsl = slice(tt * 512, (tt + 1) * 512)
                        ps, pb = self.rot()

                        def grp(e):
                            ins = None
                            for kc in range(16):
                                ins = e.matmul(ps, lhsT=w[:, kc, j * 128:(j + 1) * 128], rhs=H[:, kc, tsl], start=(kc == 0), stop=(kc == 15))
                            return ins
                        k.op("pe", grp, reads=Ht(tt) + [wb], writes=[pb])
                        if kind == 0:
                            k.op("act", lambda e: e.activation(out=Q[:, j, tsl], in_=ps, func=AF.Copy, scale=128.0 ** -0.5),
                                 reads=[pb], writes=[Qb[j][tt]])
                        else:
                            k.op("dve", lambda e: e.tensor_copy(out=Kt[:, j, tsl], in_=ps), reads=[pb], writes=[Kb[j][tt]])
            w, wb = self.load_w(T["fox_in_r1"][2 + grpi], (16, 512))
            for ts in range(16):
                ps, pb = self.rot()

                def grp(e):
                    ins = None
                    for kc in range(16):
                        ins = e.matmul(ps, lhsT=H[:, kc, ts * 128:(ts + 1) * 128], rhs=w[:, kc, :], start=(kc == 0), stop=(kc == 15))
                    return ins
                k.op("pe", grp, reads=Ht(ts // 4) + [wb], writes=[pb])
                if ts % 2 == 0:
                    k.op("act", lambda e: e.activation(out=V[:, ts, :], in_=ps, func=AF.Copy), reads=[pb], writes=[Vb[ts]])
                else:
                    k.op("dve", lambda e: e.tensor_copy(out=V[:, ts, :], in_=ps), reads=[pb], writes=[Vb[ts]])
            for j in range(4):
                h = grpi * 4 + j
                for qt in range(NT):
                    tsl = slice(qt * 512, (qt + 1) * 512)
                    cs = (h * NT + qt) % 2
                    k.op("dve", lambda e: e.tensor_scalar(out=OH[cs], in0=CUM[:, tsl], scalar1=ONEHOT[:, h:h + 1], scalar2=None, op0=ALU.mult),
                         reads=[cumB, gB], writes=[ohB[cs]])
                    k.op("pe", lambda e: e.matmul(ps7, lhsT=ONES16F, rhs=OH[cs], start=True, stop=True), reads=[ohB[cs], gB], writes=[pb7])
                    k.op("act", lambda e: e.activation(out=CQ[cs], in_=ps7, func=AF.Copy), reads=[pb7], writes=[cqB[cs]])
                    oi = (h * NT + qt) % 2
                    nk = 4 * qt + 4
                    for kc in range(nk):
                        j0 = max(0, kc - 4 * qt)
                        cols = slice(j0 * 128, 512)
                        i = state["ti"] % 3
                        state["ti"] += 1
                        psn = i
                        ps, pb = self.PS[:, psn, :], self.psB[psn]
                        k.op("pe", lambda e: e.matmul(ps[:, cols], lhsT=Kt[:, j, kc * 128:(kc + 1) * 128],
                                                      rhs=Q[:, j, qt * 512 + j0 * 128:(qt + 1) * 512], start=True, stop=True),
                             reads=[Kb[j][kc // 4], Qb[j][qt]], writes=[pb])
                        k.op("dve", lambda e: e.tensor_tensor(out=SX[i][:, cols], in0=ps[:, cols], in1=CQ[cs][:, cols], op=ALU.add),
                             reads=[pb, cqB[cs]], writes=[sxB[i]])
                        if kc >= 4 * qt:
                            dsl = slice(j0 * 128, (j0 + 1) * 128)
                            k.op("pool", lambda e: e.tensor_tensor(out=SX[i][:, dsl], in0=SX[i][:, dsl], in1=self.maskc[:], op=ALU.add),
                                 reads=[sxB[i], self.cstB], writes=[sxB[i]])
                        k.op("act", lambda e: e.activation(out=PT[i][:, cols], in_=SX[i][:, cols], func=AF.Exp,
                                                           bias=CK[:, kc * 16 + h:kc * 16 + h + 1], scale=1.0),
                             reads=[sxB[i], ckB], writes=[ptB[i]])
                        pend.append((i, h, j, qt, kc, j0, kc == nk - 1, oi))
                        if len(pend) > 2:
                            emit_pv(pend.pop(0))
            while pend:
                emit_pv(pend.pop(0))
        self.rot_i = 0

    def ln_setup(self, extra=()):
        k = self.k
        W = self.WORK
        L = {}
        L["XIN"] = [W[:, i * 512:(i + 1) * 512] for i in range(3)]
        L["TB"] = [W[:, 1536 + i * 256:1792 + i * 256].bitcast(BF16) for i in range(2)]
        L["TQ"] = [W[:, 2048 + i * 256:2304 + i * 256].bitcast(BF16) for i in range(2)]
        L["MEAN"], L["RSTD"], L["NMR"], L["VAR"] = (W[:, 2560 + i * 512:3072 + i * 512] for i in range(4))
        L["XST"] = [W[:, 4608 + i * 512:5120 + i * 512] for i in range(2)]
        L["HST"] = [W[:, 5632 + i * 256:5888 + i * 256].bitcast(BF16) for i in range(2)]
        L["xinB"], L["tbB"], L["tqB"] = k.bufs(3, "xin"), k.bufs(2, "tb"), k.bufs(2, "tq")
        L["stB"] = k.bufs(4, "st")
        L["xstB"], L["hstB"] = k.bufs(2, "xst"), k.bufs(2, "hst")
        neww = L["xinB"] + L["tbB"] + L["tqB"] + L["stB"] + L["xstB"] + L["hstB"] + list(extra)
        self.inherit(neww, self.workB)
        self.workB = neww
        L["pend"] = []
        L["xi"] = 0
        L["so"] = 0
        return L

    def ln_xin(self, L, tt, dc):
        i = dc % 3
        self.k.dma("sp", L["XIN"][i], self.XA[dc, :, tt * 512:(tt + 1) * 512], L["xinB"][i],
                   reads=[self.XAb[dc][tt]], writes=[L["xinB"][i]])

    def ln_stats_mm(self, L, dc, s):
        k = self.k

        def g(e):
            e.matmul(self.PS[:, 6, :], lhsT=self.ones_bf[:], rhs=L["TB"][s], start=(dc == 0), stop=(dc == 15))
            return e.matmul(self.PS[:, 7, :], lhsT=self.ones_bf[:], rhs=L["TQ"][s], start=(dc == 0), stop=(dc == 15))
        k.op("pe", g, reads=[L["tbB"][s], L["tqB"][s], self.cstB], writes=[self.psB[6], self.psB[7]])

    def ln_chunk(self, L, tt, dc, ps, pb, gate, Tt, TtB):
        k = self.k
        if dc == 0:
            self.ln_xin(L, tt, 0)
            self.ln_xin(L, tt, 1)
        if dc + 2 < DC:
            self.ln_xin(L, tt, dc + 2)
        i = dc % 3
        k.op("dve", lambda e: e.scalar_tensor_tensor(out=Tt[:, dc, :], in0=ps, scalar=gate[:, dc:dc + 1], in1=L["XIN"][i],
                                                     op0=ALU.mult, op1=ALU.add),
             reads=[pb, L["xinB"][i], self.vecB], writes=[TtB[dc]])
        s = dc % 2
        k.op("act", lambda e: e.activation(out=L["TB"][s], in_=Tt[:, dc, :], func=AF.Copy), reads=[TtB[dc]], writes=[L["tbB"][s]])
        k.op("act", lambda e: e.activation(out=L["TQ"][s], in_=Tt[:, dc, :], func=AF.Square), reads=[TtB[dc]], writes=[L["tqB"][s]])
        L["pend"].append((dc, s))
        if len(L["pend"]) > 1:
            self.ln_stats_mm(L, *L["pend"].pop(0))

    def ln_finish(self, L, tt, l, Tt, TtB, An, Bn, Gn, Hn, final):
        k = self.k
        while L["pend"]:
            self.ln_stats_mm(L, *L["pend"].pop(0))
        MEAN, RSTD, NMR, VAR = L["MEAN"], L["RSTD"], L["NMR"], L["VAR"]
        mB, rB, nB, vB = L["stB"]
        p6, p7 = self.PS[:, 6, :], self.PS[:, 7, :]
        k.op("dve", lambda e: e.tensor_scalar(out=MEAN, in0=p6, scalar1=1.0 / D, scalar2=None, op0=ALU.mult), reads=[self.psB[6]], writes=[mB])
        k.op("dve", lambda e: e.tensor_scalar(out=VAR, in0=p7, scalar1=1.0 / D, scalar2=None, op0=ALU.mult), reads=[self.psB[7]], writes=[vB])
        k.op("dve", lambda e: e.tensor_tensor(out=NMR, in0=MEAN, in1=MEAN, op=ALU.mult), reads=[mB], writes=[nB])
        k.op("dve", lambda e: e.scalar_tensor_tensor(out=VAR, in0=VAR, scalar=LN_EPS, in1=NMR, op0=ALU.add, op1=ALU.subtract),
             reads=[vB, nB], writes=[vB])
        k.op("act", lambda e: e.activation(out=RSTD, in_=VAR, func=AF.Sqrt), reads=[vB], writes=[rB])
        k.op("dve", lambda e: e.reciprocal(out=RSTD, in_=RSTD), reads=[rB], writes=[rB])
        k.op("dve", lambda e: e.scalar_tensor_tensor(out=NMR, in0=MEAN, scalar=-1.0, in1=RSTD, op0=ALU.mult, op1=ALU.mult),
             reads=[mB, rB], writes=[nB])
        tsl = slice(tt * 512, (tt + 1) * 512)
        A, B, G, Hh = (self.vslice(l, n) for n in (An, Bn, Gn, Hn))
        for dc in range(DC):
            x = Tt[:, dc, :]
            k.op("dve", lambda e: e.tensor_tensor(out=x, in0=x, in1=RSTD, op=ALU.mult), reads=[TtB[dc], rB], writes=[TtB[dc]])
            k.op("pool", lambda e: e.tensor_tensor(out=x, in0=x, in1=NMR, op=ALU.add), reads=[TtB[dc], nB], writes=[TtB[dc]])
            if not final:
                i = L["xi"] % 2
                L["xi"] += 1
                k.op("dve", lambda e: e.tensor_scalar(out=L["XST"][i], in0=x, scalar1=A[:, dc:dc + 1], scalar2=B[:, dc:dc + 1],
                                                      op0=ALU.mult, op1=ALU.add),
                     reads=[TtB[dc], self.vecB], writes=[L["xstB"][i]])
                k.dma("sp", self.XA[dc, :, tsl], L["XST"][i], L["xstB"][i], reads=[L["xstB"][i]], writes=[self.XAb[dc][tt]])
                k.op("act", lambda e: e.activation(out=L["HST"][i], in_=x, func=AF.Identity, bias=Hh[:, dc:dc + 1], scale=G[:, dc:dc + 1]),
                     reads=[TtB[dc], self.vecB], writes=[L["hstB"][i]])
                k.dma("sp", self.HD[dc, :, tsl], L["HST"][i], L["hstB"][i], reads=[L["hstB"][i]], writes=[self.HDb[dc][tt]])
            else:
                k.op("act", lambda e: e.activation(out=x, in_=x, func=AF.Identity, bias=Hh[:, dc:dc + 1], scale=G[:, dc:dc + 1]),
                     reads=[TtB[dc], self.vecB], writes=[TtB[dc]])
        if final:
            for s in range(4):
                for q4 in range(4):
                    bi = 4 + (s * 4 + q4) % 2
                    ps, pb = self.PS[:, bi, :], self.psB[bi]

                    def g(e):
                        ins = None
                        for i in range(4):
                            ins = e.transpose(out=ps[:, i * 128:(i + 1) * 128], in_=Tt[:, q4 * 4 + i, s * 128:(s + 1) * 128], identity=self.ident[:])
                        return ins
                    k.op("pe", g, reads=TtB[q4 * 4:q4 * 4 + 4] + [self.cstB], writes=[pb])
                    i = L["xi"] % 2
                    L["xi"] += 1
                    if i == 0:
                        k.op("act", lambda e: e.activation(out=L["XST"][i], in_=ps, func=AF.Copy), reads=[pb], writes=[L["xstB"][i]])
                    else:
                        k.op("dve", lambda e: e.tensor_copy(out=L["XST"][i], in_=ps), reads=[pb], writes=[L["xstB"][i]])
                    r0 = (tt * 4 + s) * 128
                    yb = self.k.buf()
                    k.dma("sp", self.y[r0:r0 + 128, q4 * 512:(q4 + 1) * 512], L["XST"][i], L["xstB"][i], reads=[L["xstB"][i]], writes=[yb])
                    self.allbufs.append(L["xstB"][i])

    def stage_proj(self, l, wname):
        k, T = self.k, self.T
        WO = [self.bfview(self.BIGA, u * 4096, (16, 512)) for u in range(4)]
        woB = k.bufs(4, "wo")
        self.inherit(woB, self.bigaB)
        self.bigaB = woB
        for u in range(4):
            self.do_precast(1)
            k.dma("pool", WO[u], T[wname][u], woB[u], writes=[woB[u]])
        Tt = self.f32view(self.BIGB, 0, (16, 512))
        TtB = k.bufs(16, "tt")
        self.inherit(TtB, self.bigbB)
        self.bigbB = TtB
        L = self.ln_setup()
        gate = self.vslice(l, "g1p")
        for tt in range(NT):
            tsl = slice(tt * 512, (tt + 1) * 512)
            i, rb = self.ring_slot()
            IN = self.bfview(self.RING, i * 4096, (16, 512))
            k.dma("sp", IN, self.OD[:, :, tsl].rearrange("c p t -> p c t"), rb, reads=[self.ODb[c][tt] for c in range(DC)], writes=[rb])
            for dc in range(DC):
                bi = 4 + dc % 2
                ps, pb = self.PS[:, bi, :], self.psB[bi]

                def g(e):
                    ins = None
                    for hc in range(16):
                        ins = e.matmul(ps, lhsT=WO[dc // 4][:, hc, (dc % 4) * 128:(dc % 4 + 1) * 128], rhs=IN[:, hc, :],
                                       start=(hc == 0), stop=(hc == 15))
                    return ins
                k.op("pe", g, reads=[woB[dc // 4], rb], writes=[pb])
                self.ln_chunk(L, tt, dc, ps, pb, gate, Tt, TtB)
            self.ln_finish(L, tt, l, Tt, TtB, "A1", "B1", "G2", "H2", False)

    def stage_ffn(self, l, final):
        k, T = self.k, self.T
        Tt = self.f32view(self.BIGA, 0, (16, 512))
        HT = [self.bfview(self.BIGA, 8192 + i * 4096, (16, 512)) for i in range(2)]
        TtB, htB = k.bufs(16, "tt"), k.bufs(2, "ht")
        self.inherit(TtB + htB, self.bigaB)
        self.bigaB = TtB + htB
        aT = self.bfview(self.BIGB, 0, (44, 512))
        aB = k.bufs(44, "a")
        self.inherit(aB, self.bigbB)
        self.bigbB = aB
        UP = [self.bfview(self.RING, i * 2048, (16, 256)) for i in range(3)]
        DN = [self.bfview(self.RING, 6144 + i * 2816, (44, 128)) for i in range(2)]
        upB, dnB = k.bufs(3, "upw"), k.bufs(2, "dnw")
        self.inherit(upB + dnB, self.rsB)
        W = self.WORK
        ACC = [W[:, 6144 + i * 512:6656 + i * 512] for i in range(4)]
        HAL = [W[:, 8192 + i * 176:8368 + i * 176].rearrange("p (a b) -> p a b", b=2) for i in range(2)]
        CORR = W[:, 8544:8720].rearrange("p (a b) -> p a b", b=2)
        CTMP = W[:, 8720:8896].rearrange("p (a b) -> p a b", b=2)
        accB, halB, corrB = k.bufs(4, "acc"), k.bufs(2, "hal"), k.buf("corr")
        L = self.ln_setup(extra=accB + halB + [corrB])
        CT = self.ctab[:, l * 352:(l + 1) * 352].rearrange("p (a b) -> p a b", b=4)
        gate = self.vslice(l, "g2p")

        def up_load(tt, c):
            n = tt * FC + c
            s = n % 3
            k.dma("sp", UP[s], self.UPB[l, c].rearrange("p (a b) -> p a b", b=256), upB[s], reads=[self.UPb[l][c]], writes=[upB[s]])

        def dn_load(tt, dc):
            s = (tt * DC + dc) % 2
            k.dma("sp", DN[s], self.DNB[l, dc].rearrange("p (a b) -> p a b", b=128), dnB[s], reads=[self.DNb[l][dc]], writes=[dnB[s]])

        def ht_load(tt):
            k.dma("sp", HT[tt % 2], self.HD[:, :, tt * 512:(tt + 1) * 512].rearrange("c p t -> p c t"), htB[tt % 2],
                  reads=[self.HDb[c][tt] for c in range(DC)], writes=[htB[tt % 2]])

        ht_load(0)
        up_load(0, 0)
        up_load(0, 1)
        for tt in range(NT):
            hs = tt % 2
            if tt > 0:
                hp = HAL[(tt - 1) % 2]
                hb = halB[(tt - 1) % 2]
                k.op("pool", lambda e: e.tensor_tensor(out=CORR[:, :, 0], in0=hp[:, :, 1], in1=CT[:, :, 1], op=ALU.mult), reads=[hb, self.cstB], writes=[corrB])
                k.op("pool", lambda e: e.tensor_tensor(out=CTMP[:, :, 0], in0=hp[:, :, 0], in1=CT[:, :, 0], op=ALU.mult), reads=[hb, self.cstB], writes=[corrB])
                k.op("pool", lambda e: e.tensor_tensor(out=CORR[:, :, 0], in0=CORR[:, :, 0], in1=CTMP[:, :, 0], op=ALU.add), reads=[corrB], writes=[corrB])
                k.op("pool", lambda e: e.tensor_tensor(out=CORR[:, :, 1], in0=hp[:, :, 1], in1=CT[:, :, 0], op=ALU.mult), reads=[hb, self.cstB], writes=[corrB])
            for c in range(FC):
                if c + 2 < FC:
                    up_load(tt, c + 2)
                s = (tt * FC + c) % 3
                st = c % 2
                psg, psv = self.PS[:, 2 * st, :], self.PS[:, 2 * st + 1, :]
                pbg, pbv = self.psB[2 * st], self.psB[2 * st + 1]

                def g(e):
                    for kc in range(16):
                        e.matmul(psg, lhsT=UP[s][:, kc, 0:128], rhs=HT[hs][:, kc, :], start=(kc == 0), stop=(kc == 15))
                    ins = None
                    for kc in range(16):
                        ins = e.matmul(psv, lhsT=UP[s][:, kc, 128:256], rhs=HT[hs][:, kc, :], start=(kc == 0), stop=(kc == 15))
                    return ins
                k.op("pe", g, reads=[upB[s], htB[hs]], writes=[pbg, pbv])
                for half, (ps, pb, fc) in enumerate(((psg, pbg, c), (psv, pbv, FC + c))):
                    ai = st * 2 + half
                    acc, ab = ACC[ai], accB[ai]
                    k.op("act", lambda e: e.activation(out=acc, in_=ps, func=AF.Identity, bias=CT[:, fc, 3:4], scale=CT[:, fc, 2:3]),
                         reads=[pb, self.cstB], writes=[ab])
                    if tt < NT - 1:
                        k.op("act", lambda e: e.activation(out=HAL[tt % 2][:, fc, :], in_=ps[:, 510:512], func=AF.Copy),
                             reads=[pb], writes=[halB[tt % 2]])
                    if tt > 0:
                        k.op("pool", lambda e: e.tensor_tensor(out=acc[:, 0:2], in0=acc[:, 0:2], in1=CORR[:, fc, :], op=ALU.add),
                             reads=[ab, corrB], writes=[ab])
                    k.op("dve", lambda e: e.scalar_tensor_tensor(out=acc[:, 1:512], in0=ps[:, 0:511], scalar=CT[:, fc, 1:2], in1=acc[:, 1:512],
                                                                 op0=ALU.mult, op1=ALU.add), reads=[pb, ab, self.cstB], writes=[ab])
                    k.op("dve", lambda e: e.scalar_tensor_tensor(out=acc[:, 2:512], in0=ps[:, 0:510], scalar=CT[:, fc, 0:1], in1=acc[:, 2:512],
                                                                 op0=ALU.mult, op1=ALU.add), reads=[pb, ab, self.cstB], writes=[ab])
                ag, av = ACC[st * 2], ACC[st * 2 + 1]
                k.op("act", lambda e: e.activation(out=ag, in_=ag, func=AF.Silu), reads=[accB[st * 2]], writes=[accB[st * 2]])
                k.op("pool", lambda e: e.tensor_tensor(out=aT[:, c, :], in0=ag, in1=av, op=ALU.mult),
                     reads=[accB[st * 2], accB[st * 2 + 1]], writes=[aB[c]])
            dn_load(tt, 0)
            dn_load(tt, 1)
            for dc in range(DC):
                s = (tt * DC + dc) % 2
                bi = 4 + dc % 2
                ps, pb = self.PS[:, bi, :], self.psB[bi]

                def g(e):
                    ins = None
                    for c in range(FC):
                        ins = e.matmul(ps, lhsT=DN[s][:, c, :], rhs=aT[:, c, :], start=(c == 0), stop=(c == FC - 1))
                    return ins
                k.op("pe", g, reads=aB + [dnB[s]], writes=[pb])
                if dc + 2 < DC:
                    dn_load(tt, dc + 2)
                self.ln_chunk(L, tt, dc, ps, pb, gate, Tt, TtB)
            if tt + 1 < NT:
                ht_load(tt + 1)
                up_load(tt + 1, 0)
                up_load(tt + 1, 1)
            self.ln_finish(L, tt, l, Tt, TtB, "A2", "B2", "G1n", "H1n", final)
        self.inherit(self.rsB, upB + dnB)

    def stage_swa(self):
        k, T = self.k, self.T
        H = self.bfview(self.BIGA, 0, (16, 2048))
        Hb = self.Hb
        Ht = lambda t: [Hb[c][t] for c in range(DC)]
        BB = self.BIGB
        Qg = self.bfview(BB, 0, (4, 2048))
        KD = self.bfview(BB, 4096, (4, 2048))
        V = self.bfview(BB, 8192, (16, 512))
        Qb = [[k.buf() for t in range(NT)] for j in range(4)]
        Kb = [[k.buf() for t in range(NT)] for j in range(4)]
        Vb = [k.buf() for t in range(16)]
        tmpB = k.buf("ropetmp")
        self.inherit([tmpB], self.bigbB)
        W = self.WORK
        COS, SINS = W[:, 0:2048], W[:, 2048:4096]
        QRAW = [W[:, 4096 + i * 256:4352 + i * 256].bitcast(BF16) for i in range(2)]
        QC = [W[:, 4608 + i * 512:5120 + i * 512] for i in range(2)]
        SX = [W[:, 5632 + i * 256:5888 + i * 256] for i in range(3)]
        PT = [W[:, 6400 + i * 128:6528 + i * 128].bitcast(BF16) for i in range(3)]
        R = W[:, 6784:7296]
        OST = [W[:, 7296 + i * 256:7552 + i * 256].bitcast(BF16) for i in range(2)]
        ESK = W[:, 7808:7840]
        QC2 = [W[:, 7840 + i * 512:8352 + i * 512] for i in range(2)]
        tabB, qrB, qcB, sxB, ptB = k.buf("ropetab"), k.bufs(2, "qraw"), k.bufs(2, "qc"), k.bufs(3, "sx"), k.bufs(3, "pt")
        rB, ostB, eskB, qc2B = k.buf("r"), k.bufs(2, "ost"), k.buf("esk"), k.bufs(2, "qc2")
        neww = [tabB, rB, eskB] + qrB + qcB + sxB + ptB + ostB + qc2B
        self.inherit(neww, self.workB)
        self.workB = neww
        POSI = BB[:, 8192:10240].bitcast(I32)
        POSF, ANG, T1, NI = BB[:, 6144:8192], BB[:, 0:2048], BB[:, 2048:4096], BB[:, 4096:6144].bitcast(I32)
        NF = BB[:, 10240:12288]
        PI = float(np.pi)
        C1 = 6.28125
        C2 = float(2 * np.pi - 6.28125)
        dv = lambda fn: k.op("dve", fn, reads=[tmpB, self.cstB], writes=[tmpB])
        k.dma("sp", POSI, T["posb"], tmpB, writes=[tmpB])
        k.dma("sp", ESK, T["swa_sk"], eskB, writes=[eskB])
        k.op("act", lambda e: e.activation(out=ESK, in_=ESK, func=AF.Exp), reads=[eskB], writes=[eskB])
        dv(lambda e: e.tensor_copy(out=POSF, in_=POSI))
        for which, dst in ((0, SINS), (1, COS)):
            if which == 0:
                dv(lambda e: e.tensor_scalar(out=ANG, in0=POSF, scalar1=self.ropec[:, 0:1], scalar2=None, op0=ALU.mult))
            else:
                dv(lambda e: e.tensor_scalar(out=ANG, in0=POSF, scalar1=self.ropec[:, 0:1], scalar2=PI / 2, op0=ALU.mult, op1=ALU.add))
            dv(lambda e: e.tensor_scalar(out=T1, in0=ANG, scalar1=1.0 / (2 * PI), scalar2=0.5, op0=ALU.mult, op1=ALU.add))
            dv(lambda e: e.tensor_copy(out=NI, in_=T1))
            dv(lambda e: e.tensor_copy(out=NF, in_=NI))
            dv(lambda e: e.scalar_tensor_tensor(out=T1, in0=NF, scalar=-C1, in1=ANG, op0=ALU.mult, op1=ALU.add))
            dv(lambda e: e.scalar_tensor_tensor(out=T1, in0=NF, scalar=-C2, in1=T1, op0=ALU.mult, op1=ALU.add))
            dv(lambda e: e.tensor_scalar(out=NF, in0=T1, scalar1=-PI, scalar2=2 * PI, op0=ALU.is_lt, op1=ALU.mult))
            dv(lambda e: e.tensor_tensor(out=T1, in0=T1, in1=NF, op=ALU.add))
            dv(lambda e: e.tensor_scalar(out=NF, in0=T1, scalar1=PI, scalar2=-2 * PI, op0=ALU.is_gt, op1=ALU.mult))
            dv(lambda e: e.tensor_tensor(out=T1, in0=T1, in1=NF, op=ALU.add))
            dv(lambda e: e.tensor_scalar(out=T1, in0=T1, scalar1=-PI, scalar2=PI, op0=ALU.max, op1=ALU.min))
            k.op("act", lambda e: e.activation(out=dst, in_=T1, func=AF.Sin), reads=[tmpB], writes=[tabB])
        k.op("dve", lambda e: e.tensor_scalar(out=SINS, in0=SINS, scalar1=self.ropec[:, 1:2], scalar2=None, op0=ALU.mult),
             reads=[tabB, self.cstB], writes=[tabB])
        newb = [b for r in Qb for b in r] + [b for r in Kb for b in r] + Vb
        self.inherit(newb, [tmpB] + self.bigbB)
        self.bigbB = newb
        ri = {"n": 0}

        def proj_rope(w, wb, j, tt, dst, dstB):
            tsl = slice(tt * 512, (tt + 1) * 512)
            ps, pb = self.rot()

            def grp(e):
                ins = None
                for kc in range(16):
                    ins = e.matmul(ps, lhsT=w[:, kc, j * 128:(j + 1) * 128], rhs=H[:, kc, tsl], start=(kc == 0), stop=(kc == 15))
                return ins
            k.op("pe", grp, reads=Ht(tt) + [wb], writes=[pb])
            i = ri["n"] % 2
            ri["n"] += 1
            k.op("act", lambda e: e.activation(out=QRAW[i], in_=ps, func=AF.Copy), reads=[pb], writes=[qrB[i]])
            k.op("dve", lambda e: e.tensor_tensor(out=QC[i], in0=ps, in1=COS[:, tsl], op=ALU.mult), reads=[pb, tabB], writes=[qcB[i]])
            ps2, pb2 = self.rot()
            k.op("pe", lambda e: e.matmul(ps2, lhsT=self.pmat[:], rhs=QRAW[i], start=True, stop=True), reads=[qrB[i], self.cstB], writes=[pb2])
            k.op("dve", lambda e: e.tensor_tensor(out=QC2[i], in0=ps2, in1=SINS[:, tsl], op=ALU.mult), reads=[pb2, tabB], writes=[qc2B[i]])
            k.op("pool", lambda e: e.tensor_tensor(out=dst, in0=QC[i], in1=QC2[i], op=ALU.add), reads=[qcB[i], qc2B[i]], writes=[dstB])

        w, wb = self.load_w(T["swa_in_r"][4], (16, 512))
        for g in range(4):
            for tt in range(NT):
                proj_rope(w, wb, g, tt, KD[:, g, tt * 512:(tt + 1) * 512], Kb[g][tt])
        w, wb = self.load_w(T["swa_in_r"][5], (16, 512))
        for ts in range(16):
            ps, pb = self.rot()

            def grp(e):
                ins = None
                for kc in range(16):
                    ins = e.matmul(ps, lhsT=H[:, kc, ts * 128:(ts + 1) * 128], rhs=w[:, kc, :], start=(kc == 0), stop=(kc == 15))
                return ins
            k.op("pe", grp, reads=Ht(ts // 4) + [wb], writes=[pb])
            if ts % 2 == 0:
                k.op("act", lambda e: e.activation(out=V[:, ts, :], in_=ps, func=AF.Copy), reads=[pb], writes=[Vb[ts]])
            else:
                k.op("dve", lambda e: e.tensor_copy(out=V[:, ts, :], in_=ps), reads=[pb], writes=[Vb[ts]])

        pend = []
        state = {"ti": 0, "tile": 0}

        def emit_pv(t):
            i, g, hh, e_, j, qt, kb, c0, nblk, first, last, oi, oo = t
            O, oB = self.PS[:, 3 + oi, :], self.psB[3 + oi]
            DEN, dB = self.PS[:, 5 + oi, :], self.psB[5 + oi]

            def g2(e):
                ins = None
                for b in range(nblk):
                    cs = slice(c0 + b * 128, c0 + (b + 1) * 128)
                    ps_ = slice(b * 128, (b + 1) * 128)
                    st = first and b == 0
                    e.matmul(O[:, cs], lhsT=V[:, kb, g * 128:(g + 1) * 128], rhs=PT[i][:, ps_], start=st, stop=last)
                    ins = e.matmul(DEN[:, cs], lhsT=self.ones_bf[:], rhs=PT[i][:, ps_], start=st, stop=last)
                return ins
            k.op("pe", g2, reads=[ptB[i], Vb[kb], self.cstB], writes=[oB, dB])
            if last:
                k.op("dve", lambda e: e.tensor_scalar(out=R, in0=DEN, scalar1=ESK[:, hh:hh + 1], scalar2=None, op0=ALU.add),
                     reads=[dB, eskB], writes=[rB])
                k.op("dve", lambda e: e.reciprocal(out=R, in_=R), reads=[rB], writes=[rB])
                hs = slice(e_ * 64, (e_ + 1) * 64)
                k.op("dve", lambda e: e.tensor_tensor(out=OST[oo][hs, :], in0=O[hs, :], in1=R[hs, :], op=ALU.mult),
                     reads=[oB, rB], writes=[ostB[oo]])
                if e_ == 1:
                    c = 4 * g + j
                    k.dma("sp", self.OD[c, :, qt * 512:(qt + 1) * 512], OST[oo], ostB[oo], reads=[ostB[oo]], writes=[self.ODb[c][qt]])

        for g in range(4):
            w, wb = self.load_w(T["swa_in_r"][g], (16, 512))
            for j in range(4):
                for tt in range(NT):
                    proj_rope(w, wb, j, tt, Qg[:, j, tt * 512:(tt + 1) * 512], Qb[j][tt])
            for j in range(4):
                for qt in range(NT):
                    oo = state["tile"] % 2
                    state["tile"] += 1
                    for e_ in range(2):
                        hh = 8 * g + 2 * j + e_
                        hs = slice(e_ * 64, (e_ + 1) * 64)
                        oi = (hh * NT + qt) % 2
                        kbs = list(range(max(0, 4 * qt - 1), 4 * qt + 4))
                        for n, kb in enumerate(kbs):
                            if kb == 4 * qt - 1:
                                c0, nblk, msk = 0, 1, self.masks[:, 128:256]
                            else:
                                c0 = (kb - 4 * qt) * 128
                                nblk = 2 if kb < 4 * qt + 3 else 1
                                msk = self.masks[:, 0:nblk * 128]
                            nc_ = nblk * 128
                            i = state["ti"] % 3
                            state["ti"] += 1
                            ps, pb = self.PS[:, i, :], self.psB[i]
                            k.op("pe", lambda e: e.matmul(ps[:, 0:nc_], lhsT=KD[hs, g, kb * 128:(kb + 1) * 128],
                                                          rhs=Qg[hs, j, qt * 512 + c0:qt * 512 + c0 + nc_], start=True, stop=True),
                                 reads=[Kb[g][kb // 4], Qb[j][qt]], writes=[pb])
                            k.op("dve", lambda e: e.tensor_tensor(out=SX[i][:, 0:nc_], in0=ps[:, 0:nc_], in1=msk, op=ALU.add),
                                 reads=[pb, self.cstB], writes=[sxB[i]])
                            k.op("act", lambda e: e.activation(out=PT[i][:, 0:nc_], in_=SX[i][:, 0:nc_], func=AF.Exp, scale=0.125),
                                 reads=[sxB[i]], writes=[ptB[i]])
                            pend.append((i, g, hh, e_, j, qt, kb, c0, nblk, n == 0, n == len(kbs) - 1, oi, oo))
                            if len(pend) > 2:
                                emit_pv(pend.pop(0))
            while pend:
                emit_pv(pend.pop(0))
        self.rot_i = 0


def build_program():
    p = Prog()
    p.stage_consts()
    p.stage_mod(0)
    p.stage_pro()
    p.stage_fox()
    p.stage_proj(0, "fox_o_r")
    p.stage_mod(1)
    p.stage_ffn(0, False)
    p.stage_loadH()
    p.stage_swa()
    p.stage_proj(1, "swa_o_r")
    p.stage_ffn(1, True)
    p.finish()
    return p


def kernel(**inputs):
    sh, per = prep_inputs(inputs)
    p = build_program()
    in_maps = []
    for b in range(8):
        m = dict(sh)
        m.update(per[b])
        in_maps.append(m)
    res = run_bass_kernel_spmd(p.nc, in_maps, core_ids=list(range(8)))
    return np.stack([np.asarray(res.results[b]["y"], dtype=np.float32) for b in range(8)], axis=0)
```
